# Optimizing a Trainium2 kernel written in Bass

```python
import math, functools
import jax, jax.numpy as jnp
from jax import lax
import numpy as np

D_MODEL = 1024
BATCH = 2
SEQ = 8192
DEPTH = 4

N_MIXERS = 3
N_LAYERS_CONV = len(range(0, DEPTH, N_MIXERS))
N_LAYERS_FOX = len(range(1, DEPTH, N_MIXERS))
N_LAYERS_SSD = len(range(2, DEPTH, N_MIXERS))

RMS_EPS = 1e-6

D_FF = -(-8 * D_MODEL // (3 * 256)) * 256

CONV_WIDTH = 3

ATTN_HEAD_DIM = 64
ATTN_HEADS = D_MODEL // ATTN_HEAD_DIM
ATTN_WIDTH = ATTN_HEADS * ATTN_HEAD_DIM
Q_BLOCK = 128
FOX_IN = 3 * ATTN_WIDTH + ATTN_HEADS

SSM_EXPAND = 2
SSM_D_INNER = SSM_EXPAND * D_MODEL
SSM_HEAD_DIM = 64
SSM_HEADS = SSM_D_INNER // SSM_HEAD_DIM
SSM_GROUPS = 8
SSM_HEADS_PER_GROUP = SSM_HEADS // SSM_GROUPS
SSM_STATE = 128
SSM_CONV = 4
SSM_CHUNK = 128
SSM_CONV_DIM = SSM_D_INNER + 2 * SSM_GROUPS * SSM_STATE
SSM_IN = SSM_D_INNER + SSM_CONV_DIM + SSM_HEADS

kernel_name = "hybrid_conv_fox_ssd_trunk"


def rms_norm(x, w, eps=RMS_EPS):
    xf = x.astype(jnp.float32)
    y = xf * lax.rsqrt(jnp.mean(xf * xf, axis=-1, keepdims=True) + eps)
    return (y * w.astype(jnp.float32)).astype(x.dtype)


def causal_depthwise_conv(u, w):
    k_w = w.shape[0]
    seq = u.shape[1]
    up = jnp.pad(u, ((0, 0), (k_w - 1, 0), (0, 0)))
    out = up[:, 0:seq] * w[0]
    for k in range(1, k_w):
        out = out + up[:, k:k + seq] * w[k]
    return out


def swiglu_ffn(h, w_gu, w_down):
    g, u = jnp.split(h @ w_gu, 2, axis=-1)
    return (jax.nn.silu(g) * u) @ w_down


def short_conv_mixer(h, w_in, conv_w, w_out):
    b_gate, c_gate, v = jnp.split(h @ w_in, 3, axis=-1)
    u = causal_depthwise_conv(c_gate * v, conv_w)
    return (b_gate * u) @ w_out


def forgetting_attention(h, w_in, b_f, q_gain, k_gain, w_out):
    bsz, seq, _ = h.shape
    proj = h @ w_in
    q, k, v, f_logit = jnp.split(proj, [ATTN_WIDTH, 2 * ATTN_WIDTH, 3 * ATTN_WIDTH], axis=-1)
    q = rms_norm(q.reshape(bsz, seq, ATTN_HEADS, ATTN_HEAD_DIM), q_gain).astype(jnp.float32)
    k = rms_norm(k.reshape(bsz, seq, ATTN_HEADS, ATTN_HEAD_DIM), k_gain).astype(jnp.float32)
    v = v.reshape(bsz, seq, ATTN_HEADS, ATTN_HEAD_DIM).astype(jnp.float32)
    log_f = jax.nn.log_sigmoid((f_logit + b_f).astype(jnp.float32))
    cum = jnp.cumsum(log_f, axis=1)
    cum_k = jnp.transpose(cum, (0, 2, 1))[:, :, None, :]
    n_blk = seq // Q_BLOCK
    q_blk = jnp.moveaxis(q.reshape(bsz, n_blk, Q_BLOCK, ATTN_HEADS, ATTN_HEAD_DIM), 1, 0)
    cum_q = jnp.moveaxis(cum.reshape(bsz, n_blk, Q_BLOCK, ATTN_HEADS), 1, 0)
    key_pos = jnp.arange(seq)
    scale = ATTN_HEAD_DIM ** -0.5

    def attend(args):
        qb, cq, bi = args
        logits = jnp.einsum('bqhd,bkhd->bhqk', qb, k) * scale
        logits = logits + jnp.transpose(cq, (0, 2, 1))[..., None] - cum_k
        q_pos = bi * Q_BLOCK + jnp.arange(Q_BLOCK)
        causal = q_pos[:, None] >= key_pos[None, :]
        logits = jnp.where(causal, logits, -jnp.inf)
        p = jax.nn.softmax(logits, axis=-1)
        return jnp.einsum('bhqk,bkhd->bqhd', p, v)

    out = lax.map(attend, (q_blk, cum_q, jnp.arange(n_blk)))
    out = jnp.moveaxis(out, 0, 1).reshape(bsz, seq, ATTN_WIDTH).astype(h.dtype)
    return out @ w_out


def ssd_chunked(xs, dt, a, b_m, c_m):
    bsz, seq = xs.shape[:2]
    nc = seq // SSM_CHUNK
    L, G, K, P, N = SSM_CHUNK, SSM_GROUPS, SSM_HEADS_PER_GROUP, SSM_HEAD_DIM, SSM_STATE
    x = xs.reshape(bsz, nc, L, G, K, P)
    dtc = dt.reshape(bsz, nc, L, G, K)
    bc = b_m.reshape(bsz, nc, L, G, N)
    cc = c_m.reshape(bsz, nc, L, G, N)
    acum = jnp.cumsum(dtc * a.reshape(G, K), axis=2)
    seg = acum[:, :, :, None] - acum[:, :, None]
    mask = jnp.tril(jnp.ones((L, L), dtype=bool))[:, :, None, None]
    decay = jnp.exp(jnp.where(mask, seg, -jnp.inf))
    cb = jnp.einsum('bclgn,bcsgn->bclsg', cc, bc)
    w = cb[..., None] * decay * dtc[:, :, None]
    y_diag = jnp.einsum('bclsgk,bcsgkp->bclgkp', w, x)
    decay_states = jnp.exp(acum[:, :, -1:] - acum)
    states = jnp.einsum('bclgn,bclgk,bclgkp->bcgkpn', bc, decay_states * dtc, x)
    chunk_decay = jnp.exp(acum[:, :, -1])

    def step(hst, inp):
        st, dec = inp
        return dec[..., None, None] * hst + st, hst

    h0 = jnp.zeros((bsz, G, K, P, N), jnp.float32)
    _, prev = lax.scan(step, h0, (jnp.moveaxis(states, 1, 0), jnp.moveaxis(chunk_decay, 1, 0)))
    prev = jnp.moveaxis(prev, 0, 1)
    y_off = jnp.einsum('bclgn,bcgkpn,bclgk->bclgkp', cc, prev, jnp.exp(acum))
    return (y_diag + y_off).reshape(bsz, seq, SSM_HEADS, P)


def mamba2_mixer(h, w_in, conv_w, conv_b, dt_bias, a_log, d_skip, norm_w, w_out):
    bsz, seq, _ = h.shape
    proj = h @ w_in
    z, xbc, dt = jnp.split(proj, [SSM_D_INNER, SSM_D_INNER + SSM_CONV_DIM], axis=-1)
    xbc = jax.nn.silu(causal_depthwise_conv(xbc, conv_w) + conv_b)
    xs, b_m, c_m = jnp.split(xbc, [SSM_D_INNER, SSM_D_INNER + SSM_GROUPS * SSM_STATE], axis=-1)
    xs = xs.reshape(bsz, seq, SSM_HEADS, SSM_HEAD_DIM).astype(jnp.float32)
    b_m = b_m.reshape(bsz, seq, SSM_GROUPS, SSM_STATE).astype(jnp.float32)
    c_m = c_m.reshape(bsz, seq, SSM_GROUPS, SSM_STATE).astype(jnp.float32)
    dt = jax.nn.softplus((dt + dt_bias).astype(jnp.float32))
    a = -jnp.exp(a_log.astype(jnp.float32))
    y = ssd_chunked(xs, dt, a, b_m, c_m) + d_skip.astype(jnp.float32)[:, None] * xs
    y = y.reshape(bsz, seq, SSM_D_INNER) * jax.nn.silu(z.astype(jnp.float32))
    yg = y.reshape(bsz, seq, SSM_GROUPS, SSM_D_INNER // SSM_GROUPS)
    yg = yg * lax.rsqrt(jnp.mean(yg * yg, axis=-1, keepdims=True) + RMS_EPS)
    y = (yg.reshape(bsz, seq, SSM_D_INNER) * norm_w.astype(jnp.float32)).astype(h.dtype)
    return y @ w_out


def setup_inputs(seed: int = 0) -> dict:
    key = jax.random.key(seed)
    ks = iter(jax.random.split(key, 32))
    f32 = jnp.float32

    def normal(shape, scale):
        return jax.random.normal(next(ks), shape, f32) * scale

    def gains(shape):
        return 1.0 + normal(shape, 0.02)

    out_scale = (2.0 * DEPTH) ** -0.5
    nC, nF, nS = N_LAYERS_CONV, N_LAYERS_FOX, N_LAYERS_SSD
    dt0 = jnp.exp(jax.random.uniform(next(ks), (nS, SSM_HEADS), f32, math.log(1e-3), math.log(1e-1)))
    return {
        "x": normal((BATCH, SEQ, D_MODEL), 1.0),
        "mix_norm": gains((DEPTH, D_MODEL)),
        "ffn_norm": gains((DEPTH, D_MODEL)),
        "ffn_w_gu": normal((DEPTH, D_MODEL, 2 * D_FF), D_MODEL ** -0.5),
        "ffn_w_down": normal((DEPTH, D_FF, D_MODEL), D_FF ** -0.5 * out_scale),
        "conv_w_in": normal((nC, D_MODEL, 3 * D_MODEL), D_MODEL ** -0.5),
        "conv_w_dw": normal((nC, CONV_WIDTH, D_MODEL), CONV_WIDTH ** -0.5),
        "conv_w_out": normal((nC, D_MODEL, D_MODEL), D_MODEL ** -0.5 * out_scale),
        "fox_w_in": normal((nF, D_MODEL, FOX_IN), D_MODEL ** -0.5),
        "fox_b_f": 2.0 + normal((nF, ATTN_HEADS), 0.5),
        "fox_q_gain": gains((nF, ATTN_HEAD_DIM)),
        "fox_k_gain": gains((nF, ATTN_HEAD_DIM)),
        "fox_w_out": normal((nF, ATTN_WIDTH, D_MODEL), ATTN_WIDTH ** -0.5 * out_scale),
        "ssd_w_in": normal((nS, D_MODEL, SSM_IN), D_MODEL ** -0.5),
        "ssd_conv_w": normal((nS, SSM_CONV, SSM_CONV_DIM), SSM_CONV ** -0.5),
        "ssd_conv_b": normal((nS, SSM_CONV_DIM), 0.02),
        "ssd_dt_bias": dt0 + jnp.log(-jnp.expm1(-dt0)),
        "ssd_a_log": jnp.log(jax.random.uniform(next(ks), (nS, SSM_HEADS), f32, 1.0, 16.0)),
        "ssd_d": gains((nS, SSM_HEADS)),
        "ssd_norm_w": gains((nS, SSM_D_INNER)),
        "ssd_w_out": normal((nS, SSM_D_INNER, D_MODEL), SSM_D_INNER ** -0.5 * out_scale),
    }


def reference(x, mix_norm, ffn_norm, ffn_w_gu, ffn_w_down,
              conv_w_in, conv_w_dw, conv_w_out,
              fox_w_in, fox_b_f, fox_q_gain, fox_k_gain, fox_w_out,
              ssd_w_in, ssd_conv_w, ssd_conv_b, ssd_dt_bias, ssd_a_log, ssd_d, ssd_norm_w, ssd_w_out):
    for i in range(DEPTH):
        kind, j = i % N_MIXERS, i // N_MIXERS
        h = rms_norm(x, mix_norm[i])
        if kind == 0:
            m = short_conv_mixer(h, conv_w_in[j], conv_w_dw[j], conv_w_out[j])
        elif kind == 1:
            m = forgetting_attention(h, fox_w_in[j], fox_b_f[j], fox_q_gain[j], fox_k_gain[j], fox_w_out[j])
        else:
            m = mamba2_mixer(h, ssd_w_in[j], ssd_conv_w[j], ssd_conv_b[j], ssd_dt_bias[j],
                             ssd_a_log[j], ssd_d[j], ssd_norm_w[j], ssd_w_out[j])
        x = x + m
        x = x + swiglu_ffn(rms_norm(x, ffn_norm[i]), ffn_w_gu[i], ffn_w_down[i])
    return x
```

```python
import numpy as np
import ml_dtypes
import concourse.bass as bass
import concourse.mybir as mybir
from concourse.bass_utils import run_bass_kernel_spmd

F32 = mybir.dt.float32
BF16 = mybir.dt.bfloat16
AF = mybir.ActivationFunctionType
ALU = mybir.AluOpType
AX = mybir.AxisListType

SEM_CHUNK = 8192
N_DMA_SEMS = 12


class Buf:
    __slots__ = ("name", "w", "r")

    def __init__(self, name=""):
        self.name = name
        self.w = None
        self.r = []


class Sched:
    ENGS = ("pe", "act", "dve", "pool", "sp")

    def __init__(self, same_engine_sync=True):
        self.prog = {e: [] for e in self.ENGS}
        self.count = {e: 0 for e in self.ENGS}
        self.waited = {e: {} for e in self.ENGS}
        self.dma_rr = {e: 0 for e in self.ENGS}
        self.dma_val = {}
        self.same_engine_sync = same_engine_sync
        self.max_sem_chunk = {e: 0 for e in self.ENGS}
        self.final_tokens = []

    def _need(self, eng, toks):
        need = {}
        for t in toks:
            if t is None:
                continue
            if t[0] == "e":
                if t[1] == eng and (eng == "pe" or not self.same_engine_sync):
                    continue
                key = ("e", t[1], (t[2] - 1) // SEM_CHUNK)
                val = (t[2] - 1) % SEM_CHUNK + 1
            else:
                key = ("d", t[1], t[2])
                val = t[3]
            if need.get(key, 0) < val:
                need[key] = val
        out = []
        wd = self.waited[eng]
        for key, val in need.items():
            if wd.get(key, 0) >= val:
                continue
            wd[key] = val
            out.append((key, val))
        return out

    def op(self, eng, fn, reads=(), writes=(), signal=True):
        toks = []
        for b in reads:
            toks.append(b.w)
        for b in writes:
            toks.append(b.w)
            toks.extend(b.r)
        waits = self._need(eng, toks)
        if signal:
            self.count[eng] += 1
            tok = ("e", eng, self.count[eng])
        else:
            tok = ("e", eng, self.count[eng] + 1)
        self.prog[eng].append((fn, waits, ("e", signal)))
        for b in reads:
            b.r.append(tok)
        for b in writes:
            b.w = tok
            b.r = []
        return tok

    def dma(self, eng, fn, reads=(), writes=()):
        toks = []
        for b in reads:
            toks.append(b.w)
        for b in writes:
            toks.append(b.w)
            toks.extend(b.r)
        slot = self.dma_rr[eng] % N_DMA_SEMS
        self.dma_rr[eng] += 1
        prev = self.dma_val.get((eng, slot), 0)
        if prev:
            toks.append(("d", eng, slot, prev))
        waits = self._need(eng, toks)
        val = prev + 16
        self.dma_val[(eng, slot)] = val
        tok = ("d", eng, slot, val)
        self.prog[eng].append((fn, waits, ("d", slot)))
        for b in reads:
            b.r.append(tok)
        for b in writes:
            b.w = tok
            b.r = []
        return tok

    def finish(self, eng, bufs):
        toks = [b.w for b in bufs]
        waits = self._need(eng, toks)
        self.prog[eng].append((None, waits, None))

    def emit(self, nc):
        import contextlib
        n_chunks = {e: (self.count[e] + SEM_CHUNK - 1) // SEM_CHUNK + 1 for e in self.ENGS}
        with contextlib.ExitStack() as st:
            esems = {}
            for e in self.ENGS:
                for c in range(n_chunks[e]):
                    esems[(e, c)] = st.enter_context(nc.semaphore(f"s_{e}_{c}"))
            dsems = {}
            for (e, slot) in self.dma_val:
                dsems[(e, slot)] = st.enter_context(nc.semaphore(f"d_{e}_{slot}"))
            block = st.enter_context(nc.Block())

            def run(engname, engine):
                cnt = 0
                for fn, waits, sig in self.prog[engname]:
                    for key, val in waits:
                        if key[0] == "e":
                            engine.wait_ge(esems[(key[1], key[2])], val)
                        else:
                            engine.wait_ge(dsems[(key[1], key[2])], val)
                    if fn is None:
                        continue
                    ins = fn(engine)
                    if sig[0] == "e":
                        if sig[1]:
                            ins.then_inc(esems[(engname, cnt // SEM_CHUNK)], 1)
                            cnt += 1
                    else:
                        ins.then_inc(dsems[(engname, sig[1])], 16)

            @block.tensor
            def _(eng):
                run("pe", eng)

            @block.scalar
            def _(eng):
                run("act", eng)

            @block.vector
            def _(eng):
                run("dve", eng)

            @block.gpsimd
            def _(eng):
                run("pool", eng)

            @block.sync
            def _(eng):
                run("sp", eng)


D = 1024
DFF = 2816
NKC = D // 128
NFC = DFF // 128
FFN_SEGS = [(0, 6), (6, 12), (12, 17), (17, 22)]
EPS = 1e-6


class Prog:
    def __init__(self, name="k"):
        import contextlib
        self.nc = bass.Bass("TRN2", target_bir_lowering=False)
        self.S = Sched()
        self.st = contextlib.ExitStack()
        self.n = 0
        self.psum_banks = None
        self.psum_rr = 0

    def sb(self, shape, dt, name=None):
        self.n += 1
        return self.st.enter_context(self.nc.sbuf_tensor(name or f"sb{self.n}", list(shape), dt))

    def ps(self, shape, dt, name=None):
        self.n += 1
        return self.st.enter_context(self.nc.psum_tensor(name or f"ps{self.n}", list(shape), dt))

    def dram_in(self, name, shape, dt):
        return self.nc.dram_tensor(name, list(shape), dt, kind="ExternalInput").ap()

    def dram_out(self, name, shape, dt):
        return self.nc.dram_tensor(name, list(shape), dt, kind="ExternalOutput").ap()

    def init_common(self):
        S = self.S
        self.ones = self.sb([128, 128], BF16, "ones")
        self.b_const = Buf("const")
        self.eps_t = self.sb([128, 1], F32, "eps_t")
        S.op("pool", lambda e: e.memset(self.ones[:], 1.0), writes=[self.b_const])
        S.op("pool", lambda e: e.memset(self.eps_t[:], EPS), writes=[self.b_const])
        self.psum_banks = [(self.ps([128, 512], F32, f"bank{i}"), Buf(f"bank{i}")) for i in range(8)]

    def bank(self):
        b = self.psum_banks[self.psum_rr % 8]
        self.psum_rr += 1
        return b

    def finish(self, out_bufs):
        self.S.finish("sp", out_bufs)
        self.S.emit(self.nc)
        self.st.close()
        return self.nc


class Rot:
    def __init__(self, prog, n, shape, dt, name):
        self.tiles = [(prog.sb(shape, dt, f"{name}{i}"), Buf(f"{name}{i}")) for i in range(n)]
        self.i = 0

    def next(self):
        t = self.tiles[self.i % len(self.tiles)]
        self.i += 1
        return t


class WLoader:
    def __init__(self, prog, max_elems, nstage=2, nbf=3, name="w", cast_eng="pool"):
        self.p = prog
        self.stage = Rot(prog, nstage, [128, max_elems], F32, name + "_st")
        self.bf = Rot(prog, nbf, [128, max_elems], BF16, name + "_bf")
        self.cast_eng = cast_eng

    def load(self, views):
        S = self.p.S
        st, b_st = self.stage.next()
        off = 0
        for v in views:
            shp = v.shape
            n = int(np.prod(shp[1:]))
            dst = st[:, off:off + n]
            if len(shp) == 3:
                dst = dst.rearrange("p (a b) -> p a b", a=shp[1])
            elif len(shp) == 4:
                dst = dst.rearrange("p (a b c) -> p a b c", a=shp[1], b=shp[2])
            S.dma("sp", lambda e, dst=dst, v=v: e.dma_start(out=dst, in_=v), writes=[b_st])
            off += n
        bf, b_bf = self.bf.next()
        S.op(self.cast_eng, lambda e, bf=bf, st=st, off=off: e.tensor_copy(out=bf[:, 0:off], in_=st[:, 0:off]),
             reads=[b_st], writes=[b_bf])
        return bf, b_bf


def rmsnorm(p, x_sb, b_x, g_col, b_g, h_out, b_h, ntok, sq_rot, rstd_rot):
    S = p.S
    for t0 in range(0, ntok, 512):
        n = min(512, ntok - t0)
        sq, b_sq = sq_rot.next()
        S.op("act", lambda e, sq=sq, t0=t0, n=n: e.activation(out=sq[:, :, 0:n], in_=x_sb[:, :, t0:t0 + n], func=AF.Square),
             reads=[b_x], writes=[b_sq])
        bank, b_bank = p.bank()
        for c in range(NKC):
            S.op("pe", lambda e, c=c, sq=sq, bank=bank, n=n: e.matmul(bank[:, 0:n], lhsT=p.ones[:], rhs=sq[:, c, 0:n], start=(c == 0), stop=(c == NKC - 1)),
                 reads=[p.b_const, b_sq], writes=[b_bank], signal=(c == NKC - 1))
        rstd, b_rstd = rstd_rot.next()
        S.op("act", lambda e, rstd=rstd, bank=bank, n=n: e.activation(out=rstd[:, 0:n], in_=bank[:, 0:n], func=AF.Sqrt, bias=p.eps_t[:], scale=1.0 / D),
             reads=[b_bank, p.b_const], writes=[b_rstd])
        S.op("dve", lambda e, rstd=rstd, n=n: e.reciprocal(out=rstd[:, 0:n], in_=rstd[:, 0:n]), reads=[b_rstd], writes=[b_rstd])
        for c in range(NKC):
            S.op("dve", lambda e, c=c, rstd=rstd, t0=t0, n=n: e.scalar_tensor_tensor(
                out=h_out[:, c, t0:t0 + n], in0=x_sb[:, c, t0:t0 + n], scalar=g_col[:, c:c + 1], in1=rstd[:, 0:n],
                op0=ALU.mult, op1=ALU.mult), reads=[b_x, b_g, b_rstd], writes=[b_h])


def gemm_acc(p, w_bf, b_w, w_of, rhs_of, nk, n, reads):
    S = p.S
    bank, b_bank = p.bank()
    for k in range(nk):
        lhs = w_of(k)
        rhs = rhs_of(k)
        S.op("pe", lambda e, k=k, bank=bank, lhs=lhs, rhs=rhs, n=n: e.matmul(bank[0:lhs.shape[-1], 0:n], lhsT=lhs, rhs=rhs, start=(k == 0), stop=(k == nk - 1)),
             reads=[b_w] + list(reads), writes=[b_bank], signal=(k == nk - 1))
    return bank, b_bank


def ffn_pass(p, wl, wlr, x_sb, b_x, h, b_h, w_gu, w_down, ntok, hid, b_hid, silu_rot):
    S = p.S
    tiles = [(t0, min(512, ntok - t0)) for t0 in range(0, ntok, 512)]
    for (c0, c1) in FFN_SEGS:
        for c in range(c0, c1):
            vg = w_gu[:, c * 128:(c + 1) * 128].rearrange("(k q) n -> q k n", q=128)
            vu = w_gu[:, DFF + c * 128:DFF + (c + 1) * 128].rearrange("(k q) n -> q k n", q=128)
            wbf, b_w = wl.load([vg, vu])
            for (t0, n) in tiles:
                pg, b_pg = gemm_acc(p, wbf, b_w, lambda k: wbf[:, k * 128:(k + 1) * 128], lambda k: h[:, k, t0:t0 + n], NKC, n, [b_h])
                pu, b_pu = gemm_acc(p, wbf, b_w, lambda k: wbf[:, 1024 + k * 128:1024 + (k + 1) * 128], lambda k: h[:, k, t0:t0 + n], NKC, n, [b_h])
                sg, b_sg = silu_rot.next()
                S.op("act", lambda e, sg=sg, pg=pg, n=n: e.activation(out=sg[:, 0:n], in_=pg[:, 0:n], func=AF.Silu), reads=[b_pg], writes=[b_sg])
                S.op("dve", lambda e, sg=sg, pu=pu, ci=c - c0, t0=t0, n=n: e.tensor_tensor(out=hid[:, ci, t0:t0 + n], in0=sg[:, 0:n], in1=pu[:, 0:n], op=ALU.mult),
                     reads=[b_sg, b_pu], writes=[b_hid])
        nseg = c1 - c0
        wds = []
        for c in range(c0, c1):
            wd, b_wd = wlr.load([w_down[c * 128:(c + 1) * 128, :]])
            wds.append((wd, b_wd))
        for m in range(NKC):
            for (t0, n) in tiles:
                bank, b_bank = p.bank()
                for ci in range(nseg):
                    wd, b_wd = wds[ci]
                    S.op("pe", lambda e, ci=ci, wd=wd, bank=bank, m=m, t0=t0, n=n: e.matmul(
                        bank[:, 0:n], lhsT=wd[:, m * 128:(m + 1) * 128], rhs=hid[:, ci, t0:t0 + n], start=(ci == 0), stop=(ci == nseg - 1)),
                        reads=[b_wd, b_hid], writes=[b_bank], signal=(ci == nseg - 1))
                S.op("dve", lambda e, bank=bank, m=m, t0=t0, n=n: e.tensor_tensor(out=x_sb[:, m, t0:t0 + n], in0=x_sb[:, m, t0:t0 + n], in1=bank[:, 0:n], op=ALU.add),
                     reads=[b_bank, b_x], writes=[b_x])


def outproj_pass(p, wl, x_sb, b_x, z, b_z, w_out, nkc, ntok):
    S = p.S
    tiles = [(t0, min(512, ntok - t0)) for t0 in range(0, ntok, 512)]
    ws = []
    for k in range(nkc):
        ws.append(wl.load([w_out[k * 128:(k + 1) * 128, :]]))
    for m in range(NKC):
        for (t0, n) in tiles:
            bank, b_bank = p.bank()
            for k in range(nkc):
                wd, b_wd = ws[k]
                S.op("pe", lambda e, k=k, wd=wd, bank=bank, m=m, t0=t0, n=n: e.matmul(
                    bank[:, 0:n], lhsT=wd[:, m * 128:(m + 1) * 128], rhs=z[:, k, t0:t0 + n], start=(k == 0), stop=(k == nkc - 1)),
                    reads=[b_wd, b_z], writes=[b_bank], signal=(k == nkc - 1))
            S.op("dve", lambda e, bank=bank, m=m, t0=t0, n=n: e.tensor_tensor(out=x_sb[:, m, t0:t0 + n], in0=x_sb[:, m, t0:t0 + n], in1=bank[:, 0:n], op=ALU.add),
                 reads=[b_bank, b_x], writes=[b_x])


def conv_mixer_pass(p, wl, h, b_h, w_in, dw_col, b_dw, ntok, cvh_in, b_cvh_in, cvh_out, b_cvh_out, z, b_z, cv_rot, u_rot, c_rot):
    S = p.S
    tiles = [(t0, min(512, ntok - t0)) for t0 in range(0, ntok, 512)]
    for j in range(NKC):
        views = [w_in[:, part * D + j * 128: part * D + (j + 1) * 128].rearrange("(k q) n -> q k n", q=128) for part in range(3)]
        wbf, b_w = wl.load(views)
        cv, b_cv = cv_rot.next()
        S.op("pool", lambda e, cv=cv, j=j: e.tensor_copy(out=cv[:, 0:2], in_=cvh_in[:, j, :]), reads=[b_cvh_in], writes=[b_cv])
        for (t0, n) in tiles:
            pc, b_pc = gemm_acc(p, wbf, b_w, lambda k: wbf[:, 1024 + k * 128:1024 + (k + 1) * 128], lambda k: h[:, k, t0:t0 + n], NKC, n, [b_h])
            pv, b_pv = gemm_acc(p, wbf, b_w, lambda k: wbf[:, 2048 + k * 128:2048 + (k + 1) * 128], lambda k: h[:, k, t0:t0 + n], NKC, n, [b_h])
            cs, b_cs = c_rot.next()
            S.op("act", lambda e, cs=cs, pc=pc, n=n: e.activation(out=cs[:, 0:n], in_=pc[:, 0:n], func=AF.Copy), reads=[b_pc], writes=[b_cs])
            S.op("dve", lambda e, cs=cs, pv=pv, cv=cv, t0=t0, n=n: e.tensor_tensor(out=cv[:, 2 + t0:2 + t0 + n], in0=cs[:, 0:n], in1=pv[:, 0:n], op=ALU.mult),
                 reads=[b_cs, b_pv], writes=[b_cv])
        u, b_u = u_rot.next()
        S.op("pool", lambda e, u=u, cv=cv, j=j: e.tensor_scalar(out=u[:, 0:ntok], in0=cv[:, 0:ntok], scalar1=dw_col[:, 0, j:j + 1], scalar2=None, op0=ALU.mult),
             reads=[b_cv, b_dw], writes=[b_u])
        for tap in (1, 2):
            S.op("dve", lambda e, u=u, cv=cv, j=j, tap=tap: e.scalar_tensor_tensor(
                out=u[:, 0:ntok], in0=cv[:, tap:tap + ntok], scalar=dw_col[:, tap, j:j + 1], in1=u[:, 0:ntok], op0=ALU.mult, op1=ALU.add),
                reads=[b_cv, b_dw, b_u], writes=[b_u])
        S.op("pool", lambda e, cv=cv, j=j: e.tensor_copy(out=cvh_out[:, j, :], in_=cv[:, ntok:ntok + 2]), reads=[b_cv], writes=[b_cvh_out])
        for (t0, n) in tiles:
            pb, b_pb = gemm_acc(p, wbf, b_w, lambda k: wbf[:, k * 128:(k + 1) * 128], lambda k: h[:, k, t0:t0 + n], NKC, n, [b_h])
            S.op("dve", lambda e, pb=pb, u=u, j=j, t0=t0, n=n: e.tensor_tensor(out=z[:, j, t0:t0 + n], in0=u[:, t0:t0 + n], in1=pb[:, 0:n], op=ALU.mult),
                 reads=[b_pb, b_u], writes=[b_z])


def conv_halo_cv(p, wl, hh, b_hh, w_in, cvh, b_cvh, c_rot):
    S = p.S
    for j in range(NKC):
        views = [w_in[:, part * D + j * 128: part * D + (j + 1) * 128].rearrange("(k q) n -> q k n", q=128) for part in (1, 2)]
        wbf, b_w = wl.load(views)
        pc, b_pc = gemm_acc(p, wbf, b_w, lambda k: wbf[:, k * 128:(k + 1) * 128], lambda k: hh[:, k, :], NKC, 2, [b_hh])
        pv, b_pv = gemm_acc(p, wbf, b_w, lambda k: wbf[:, 1024 + k * 128:1024 + (k + 1) * 128], lambda k: hh[:, k, :], NKC, 2, [b_hh])
        cs, b_cs = c_rot.next()
        S.op("act", lambda e, cs=cs, pc=pc: e.activation(out=cs[:, 0:2], in_=pc[:, 0:2], func=AF.Copy), reads=[b_pc], writes=[b_cs])
        S.op("dve", lambda e, cs=cs, pv=pv, j=j: e.tensor_tensor(out=cvh[:, j, :], in0=cs[:, 0:2], in1=pv[:, 0:2], op=ALU.mult),
             reads=[b_cs, b_pv], writes=[b_cvh])


T_CORE = 2048
TB = 1024
TBX = TB + 4


class TokState:
    def __init__(self, p):
        self.x = p.sb([128, NKC, TBX], F32, "x_sb"); self.b_x = Buf("x")
        self.h = p.sb([128, NKC, TBX], BF16, "h_sb"); self.b_h = Buf("h")
        self.zh = p.sb([128, NKC, TBX], BF16, "zh_sb"); self.b_zh = Buf("zh")
        self.cv_rot = Rot(p, 2, [128, 2 + TBX], F32, "cv")
        self.u_rot = Rot(p, 2, [128, TBX], F32, "u")
        self.c_rot = Rot(p, 2, [128, 512], F32, "cs")
        self.silu_rot = self.c_rot
        self.sq_rot = Rot(p, 1, [128, NKC, 512], BF16, "sq")
        self.rstd_rot = Rot(p, 2, [128, 512], F32, "rstd")
        self.wl = WLoader(p, 3072, nstage=2, nbf=2, name="wc")
        self.wlr = WLoader(p, 1024, nstage=2, nbf=9, name="wr")

    @property
    def hid(self):
        return self.zh


def load_small(p, dram_ap, shape, name):
    t = p.sb(shape, F32, name)
    b = Buf(name)
    p.S.dma("sp", lambda e: e.dma_start(out=t[:], in_=dram_ap), writes=[b])
    return t, b


def build_p0(debug=False):
    p = Prog()
    if debug:
        dbg_h = p.dram_out("dbg_h", [D, TB], BF16)
        dbg_z = p.dram_out("dbg_z", [D, TB], BF16)
        dbg_x = p.dram_out("dbg_x", [D, TB], F32)
        dbg_h2 = p.dram_out("dbg_h2", [D, TB], BF16)
    xT = p.dram_in("xT", [D, T_CORE], F32)
    xh = p.dram_in("xh", [D, 2], F32)
    gcols = p.dram_in("gcols", [128, 3, NKC], F32)
    dwc = p.dram_in("dwc", [128, 3, NKC], F32)
    w_in = p.dram_in("w_in", [D, 3 * D], F32)
    w_out = p.dram_in("w_out", [D, D], F32)
    w_gu = p.dram_in("w_gu", [D, 2 * DFF], F32)
    w_down = p.dram_in("w_down", [DFF, D], F32)
    x1T = p.dram_out("x1T", [D, T_CORE], F32)
    fw_in = p.dram_in("fox_w_in", [D, 3 * D + NH], F32)
    gq_d = p.dram_in("gq", [128, 1], F32)
    gk_d = p.dram_in("gk", [128, 1], F32)
    bf_d = p.dram_in("bf", [NH, 1], F32)
    S = p.S
    p.init_common()
    ts = TokState(p)
    fx = fox_inproj_setup(p, fw_in, gq_d, gk_d, bf_d)
    g_sb, b_g = load_small(p, gcols, [128, 3, NKC], "g_sb")
    dw_sb, b_dw = load_small(p, dwc, [128, 3, NKC], "dw_sb")
    xh_sb = p.sb([128, NKC, 2], F32, "xh_sb"); b_xh = Buf()
    S.dma("sp", lambda e: e.dma_start(out=xh_sb[:], in_=xh.rearrange("(c q) t -> q c t", q=128)), writes=[b_xh])
    hh = p.sb([128, NKC, 2], BF16, "hh"); b_hh = Buf()
    rmsnorm(p, xh_sb, b_xh, g_sb[:, 0, :], b_g, hh, b_hh, 2, ts.sq_rot, ts.rstd_rot)
    cvh = [(p.sb([128, NKC, 2], F32, f"cvh{i}"), Buf()) for i in range(2)]
    conv_halo_cv(p, ts.wl, hh, b_hh, w_in, cvh[0][0], cvh[0][1], ts.c_rot)
    b_o1, b_o2 = Buf(), Buf()
    for ps_ in range(T_CORE // TB):
        tok = slice(ps_ * TB, (ps_ + 1) * TB)
        S.dma("sp", lambda e, tok=tok: e.dma_start(out=ts.x[:, :, 0:TB], in_=xT[:, tok].rearrange("(c q) t -> q c t", q=128)), writes=[ts.b_x])
        rmsnorm(p, ts.x, ts.b_x, g_sb[:, 0, :], b_g, ts.h, ts.b_h, TB, ts.sq_rot, ts.rstd_rot)
        ci, co = cvh[ps_ % 2], cvh[(ps_ + 1) % 2]
        conv_mixer_pass(p, ts.wl, ts.h, ts.b_h, w_in, dw_sb, b_dw, TB, ci[0], ci[1], co[0], co[1], ts.zh, ts.b_zh,
                        ts.cv_rot, ts.u_rot, ts.c_rot)
        if debug and ps_ == 0:
            S.dma("sp", lambda e: e.dma_start(out=dbg_h.rearrange("(c q) t -> q c t", q=128), in_=ts.h[:, :, 0:TB]), reads=[ts.b_h], writes=[b_o1])
            S.dma("sp", lambda e: e.dma_start(out=dbg_z.rearrange("(c q) t -> q c t", q=128), in_=ts.zh[:, :, 0:TB]), reads=[ts.b_zh], writes=[b_o1])
        outproj_pass(p, ts.wlr, ts.x, ts.b_x, ts.zh, ts.b_zh, w_out, NKC, TB)
        if debug and ps_ == 0:
            S.dma("sp", lambda e: e.dma_start(out=dbg_x.rearrange("(c q) t -> q c t", q=128), in_=ts.x[:, :, 0:TB]), reads=[ts.b_x], writes=[b_o1])
        rmsnorm(p, ts.x, ts.b_x, g_sb[:, 1, :], b_g, ts.h, ts.b_h, TB, ts.sq_rot, ts.rstd_rot)
        if debug and ps_ == 0:
            S.dma("sp", lambda e: e.dma_start(out=dbg_h2.rearrange("(c q) t -> q c t", q=128), in_=ts.h[:, :, 0:TB]), reads=[ts.b_h], writes=[b_o1])
        ffn_pass(p, ts.wl, ts.wlr, ts.x, ts.b_x, ts.h, ts.b_h, w_gu, w_down, TB, ts.hid, ts.b_zh, ts.silu_rot)
        rmsnorm(p, ts.x, ts.b_x, g_sb[:, 2, :], b_g, ts.h, ts.b_h, TB, ts.sq_rot, ts.rstd_rot)
        S.dma("sp", lambda e, tok=tok: e.dma_start(out=x1T[:, tok].rearrange("(c q) t -> q c t", q=128), in_=ts.x[:, :, 0:TB]), reads=[ts.b_x], writes=[b_o1])
        fox_inproj_pass(p, ts, fx, tok, TB)
    return p.finish([b_o1, fx["b_out"]])


def gcol(v):
    return np.ascontiguousarray(np.asarray(v, np.float32).reshape(NKC, 128).T)


NH = 16
HD = 64


def fox_inproj_pass(p, ts, fx, tok, ntok):
    S = p.S
    tiles = [(t0, min(512, ntok - t0)) for t0 in range(0, ntok, 512)]
    w_in = fx["w_in"]
    for m in range(24):
        wbf, b_w = ts.wl.load([w_in[:, m * 128:(m + 1) * 128].rearrange("(k q) n -> q k n", q=128)])
        kind = m // 8
        for (t0, n) in tiles:
            pq, b_pq = gemm_acc(p, wbf, b_w, lambda k: wbf[:, k * 128:(k + 1) * 128], lambda k: ts.h[:, k, t0:t0 + n], NKC, n, [ts.b_h])
            o, b_o = fx["o_rot"].next()
            if kind == 2:
                S.op("act", lambda e, o=o, pq=pq, n=n: e.activation(out=o[:, 0:n], in_=pq[:, 0:n], func=AF.Copy), reads=[b_pq], writes=[b_o])
                dst = fx["vT"]
            else:
                sq, b_sq = fx["sq_rot"].next()
                S.op("act", lambda e, sq=sq, pq=pq, n=n: e.activation(out=sq[:, 0:n], in_=pq[:, 0:n], func=AF.Square), reads=[b_pq], writes=[b_sq])
                bank, b_bank = p.bank()
                S.op("pe", lambda e, bank=bank, sq=sq, n=n: e.matmul(bank[:, 0:n], lhsT=fx["blockones"][:], rhs=sq[:, 0:n], start=True, stop=True),
                     reads=[fx["b_c"], b_sq], writes=[b_bank])
                rs, b_rs = ts.rstd_rot.next()
                S.op("act", lambda e, rs=rs, bank=bank, n=n: e.activation(out=rs[:, 0:n], in_=bank[:, 0:n], func=AF.Sqrt, bias=p.eps_t[:], scale=1.0 / HD),
                     reads=[b_bank, p.b_const], writes=[b_rs])
                S.op("dve", lambda e, rs=rs, n=n: e.reciprocal(out=rs[:, 0:n], in_=rs[:, 0:n]), reads=[b_rs], writes=[b_rs])
                g = fx["gq"] if kind == 0 else fx["gk"]
                S.op("dve", lambda e, o=o, pq=pq, rs=rs, g=g, n=n: e.scalar_tensor_tensor(out=o[:, 0:n], in0=pq[:, 0:n], scalar=g[:, 0:1], in1=rs[:, 0:n], op0=ALU.mult, op1=ALU.mult),
                     reads=[b_pq, b_rs, fx["b_c"]], writes=[b_o])
                dst = fx["qT"] if kind == 0 else fx["kT"]
            mm = m % 8
            S.dma("sp", lambda e, o=o, dst=dst, mm=mm, t0=t0, n=n: e.dma_start(out=dst[mm * 128:(mm + 1) * 128, tok.start + t0:tok.start + t0 + n], in_=o[:, 0:n]),
                  reads=[b_o], writes=[fx["b_out"]])
    wbf, b_w = ts.wl.load([w_in[:, 3072:3088].rearrange("(k q) n -> q k n", q=128)])
    for (t0, n) in tiles:
        pf, b_pf = gemm_acc(p, wbf, b_w, lambda k: wbf[:, k * 16:(k + 1) * 16], lambda k: ts.h[:, k, t0:t0 + n], NKC, n, [ts.b_h])
        e1, b_e1 = ts.rstd_rot.next()
        S.op("act", lambda e, e1=e1, pf=pf, n=n: e.activation(out=e1[0:16, 0:n], in_=pf[0:16, 0:n], func=AF.Exp, bias=fx["negb"][:], scale=-1.0),
             reads=[b_pf, fx["b_c"]], writes=[b_e1])
        S.op("act", lambda e, e1=e1, n=n: e.activation(out=e1[0:16, 0:n], in_=e1[0:16, 0:n], func=AF.Ln, bias=fx["one_col"][0:16, :], scale=1.0),
             reads=[b_e1, fx["b_c"]], writes=[b_e1])
        S.dma("sp", lambda e, e1=e1, t0=t0, n=n: e.dma_start(out=fx["nlf"][:, tok.start + t0:tok.start + t0 + n], in_=e1[0:16, 0:n]),
              reads=[b_e1], writes=[fx["b_out"]])


def fox_inproj_setup(p, w_in, gq_d, gk_d, bf_d):
    S = p.S
    fx = {"w_in": w_in}
    fx["qT"] = p.dram_out("qT", [D, T_CORE], BF16)
    fx["kT"] = p.dram_out("kT", [D, T_CORE], BF16)
    fx["vT"] = p.dram_out("vT", [D, T_CORE], BF16)
    fx["nlf"] = p.dram_out("nlf", [NH, T_CORE], F32)
    fx["b_out"] = Buf("fxout")
    fx["b_c"] = Buf("fxc")
    b_c = fx["b_c"]
    fx["o_rot"] = Rot(p, 3, [128, 512], BF16, "fxo")
    fx["sq_rot"] = Rot(p, 2, [128, 512], BF16, "fxsq")
    bo = p.sb([128, 128], BF16, "blockones")
    S.op("pool", lambda e: e.memset(bo[:], 0.0), writes=[b_c])
    S.op("pool", lambda e: e.memset(bo[0:64, 0:64], 1.0), writes=[b_c])
    S.op("pool", lambda e: e.memset(bo[64:128, 64:128], 1.0), writes=[b_c])
    fx["blockones"] = bo
    gq = p.sb([128, 1], F32, "gq_sb"); gk = p.sb([128, 1], F32, "gk_sb"); negb = p.sb([16, 1], F32, "negb_sb")
    one_col = p.sb([128, 1], F32, "one_col")
    S.dma("sp", lambda e: e.dma_start(out=gq[:], in_=gq_d), writes=[b_c])
    S.dma("sp", lambda e: e.dma_start(out=gk[:], in_=gk_d), writes=[b_c])
    S.dma("sp", lambda e: e.dma_start(out=negb[:], in_=bf_d), writes=[b_c])
    S.op("dve", lambda e: e.tensor_scalar(out=gq[:], in0=gq[:], scalar1=HD ** -0.5, scalar2=0.0, op0=ALU.mult, op1=ALU.add), reads=[b_c], writes=[b_c])
    S.op("dve", lambda e: e.tensor_scalar(out=negb[:], in0=negb[:], scalar1=-1.0, scalar2=0.0, op0=ALU.mult, op1=ALU.add), reads=[b_c], writes=[b_c])
    S.op("pool", lambda e: e.memset(one_col[:], 1.0), writes=[b_c])
    fx.update(gq=gq, gk=gk, negb=negb, one_col=one_col)
    return fx


SEQ = 8192
NKB = SEQ // 128
NQB = SEQ // 512
HPC = 4
MASKNEG = -30000.0


def build_p1b(debug=False):
    p = Prog()
    S = p.S
    if debug:
        dbg_off = p.dram_out("dbg_off", [NKB, HPC], F32)
        dbg_tot = p.dram_out("dbg_tot", [NKB, HPC], F32)
        dbg_bias = p.dram_out("dbg_bias", [128, HPC * NKB], F32)
        dbg_ncum = p.dram_out("dbg_ncum", [NKB, HPC * 128], F32)
    q4 = p.dram_in("q4", [HPC, HD, SEQ], BF16)
    k4 = p.dram_in("k4", [HPC, HD, SEQ], BF16)
    v4 = p.dram_in("v4", [128, NKB, HPC, HD], BF16)
    nlf_d = p.dram_in("nlf_blk", [NKB, HPC, 128], F32)
    oT = p.dram_out("oT", [HPC * HD, SEQ], BF16)
    aug_d = p.nc.dram_tensor("aug_scratch", [HPC, 3, SEQ], BF16, kind="Internal").ap()
    b_aug_d = Buf("aug_d")
    p.init_common()
    b_c = Buf("c1b")
    ident = p.sb([128, 128], F32, "ident")
    S.op("pool", lambda e: e.memset(ident[:], 0.0), writes=[b_c])
    S.op("pool", lambda e: e.affine_select(out=ident[:], in_=ident[:], pattern=[[-1, 128]], compare_op=ALU.not_equal, fill=1.0, base=0, channel_multiplier=1), reads=[b_c], writes=[b_c])
    maskneg = p.sb([128, 128], F32, "maskneg")
    S.op("pool", lambda e: e.memset(maskneg[:], 0.0), writes=[b_c])
    S.op("pool", lambda e: e.affine_select(out=maskneg[:], in_=maskneg[:], pattern=[[1, 128]], compare_op=ALU.is_ge, fill=MASKNEG, base=0, channel_multiplier=-1), reads=[b_c], writes=[b_c])
    ltri = p.sb([64, 64], F32, "ltri")
    S.op("pool", lambda e: e.memset(ltri[:], 1.0), writes=[b_c])
    S.op("pool", lambda e: e.affine_select(out=ltri[:], in_=ltri[:], pattern=[[1, 64]], compare_op=ALU.is_ge, fill=0.0, base=-1, channel_multiplier=-1), reads=[b_c], writes=[b_c])
    onesf = p.sb([64, 128], F32, "onesf")
    S.op("pool", lambda e: e.memset(onesf[:], 1.0), writes=[b_c])
    vaug = p.sb([128, NKB, HPC, 128], BF16, "vaug"); b_v = Buf("vaug")
    S.op("pool", lambda e: e.memset(vaug[:], 1.0), writes=[b_v])
    for g in range(16):
        S.dma("sp", lambda e, g=g: e.dma_start(out=vaug[:, g * 4:(g + 1) * 4, :, 0:HD], in_=v4[:, g * 4:(g + 1) * 4, :, :]), writes=[b_v])
    nlf = p.sb([NKB, HPC, 128], F32, "nlf_sb"); b_n = Buf("nlf")
    S.dma("sp", lambda e: e.dma_start(out=nlf[:], in_=nlf_d), writes=[b_n])
    ncw = p.sb([NKB, HPC, 128], F32, "ncw"); b_ncw = Buf("ncw")
    for h in range(HPC):
        S.op("dve", lambda e, h=h: e.tensor_tensor_scan(out=ncw[:, h, :], data0=onesf[:, :], data1=nlf[:, h, :], initial=0.0, op0=ALU.mult, op1=ALU.add),
             reads=[b_n, b_c], writes=[b_ncw])
    tot = p.sb([NKB, HPC], F32, "tot"); b_tot = Buf("tot")
    S.op("dve", lambda e: e.tensor_copy(out=tot[:], in_=ncw[:, :, 127]), reads=[b_ncw], writes=[b_tot])
    misc, b_misc = p.psum_banks[6]
    S.op("pe", lambda e: e.matmul(misc[0:64, 0:HPC], lhsT=ltri[:], rhs=tot[:], start=True, stop=True), reads=[b_c, b_tot], writes=[b_misc])
    off = p.sb([NKB, HPC], F32, "off"); b_off = Buf("off")
    S.op("dve", lambda e: e.tensor_copy(out=off[:], in_=misc[0:64, 0:HPC]), reads=[b_misc], writes=[b_off])
    ncum = p.sb([NKB, HPC, 128], F32, "ncum"); b_nc = Buf("ncum")
    for h in range(HPC):
        S.op("dve", lambda e, h=h: e.tensor_scalar(out=ncum[:, h, :], in0=ncw[:, h, :], scalar1=off[:, h:h + 1], scalar2=0.0, op0=ALU.add, op1=ALU.add),
             reads=[b_ncw, b_off], writes=[b_nc])
    biascol = p.sb([128, HPC, NKB], F32, "biascol"); b_bias = Buf("bias")
    tb_, b_tb = p.psum_banks[7]
    for h in range(HPC):
        S.op("pe", lambda e, h=h: e.transpose(tb_[:, h * NKB:(h + 1) * NKB], ncum[:, h, :], ident[0:64, 0:64]), reads=[b_nc, b_c], writes=[b_tb])
    S.op("dve", lambda e: e.tensor_copy(out=biascol[:].rearrange("p h k -> p (h k)"), in_=tb_[:, 0:HPC * NKB]), reads=[b_tb], writes=[b_bias])
    if debug:
        b_dbg = Buf()
        S.dma("sp", lambda e: e.dma_start(out=dbg_off, in_=off[:]), reads=[b_off], writes=[b_dbg])
        S.dma("sp", lambda e: e.dma_start(out=dbg_tot, in_=tot[:]), reads=[b_tot], writes=[b_dbg])
        S.dma("sp", lambda e: e.dma_start(out=dbg_bias, in_=biascol[:].rearrange("p h k -> p (h k)")), reads=[b_bias], writes=[b_dbg])
        S.dma("sp", lambda e: e.dma_start(out=dbg_ncum, in_=ncum[:].rearrange("p h k -> p (h k)")), reads=[b_nc], writes=[b_dbg])
    hml = p.sb([NKB, HPC, 3, 128], BF16, "hml"); b_hml = Buf("hml")
    r_a = p.sb([NKB, HPC, 128], F32, "r_a"); r_b = p.sb([NKB, HPC, 128], F32, "r_b"); b_r = Buf("r")
    S.op("dve", lambda e: e.tensor_scalar(out=r_a[:], in0=ncum[:], scalar1=-1.0, scalar2=0.0, op0=ALU.mult, op1=ALU.add), reads=[b_nc], writes=[b_r])
    for r in range(3):
        S.op("dve", lambda e, r=r: e.tensor_copy(out=hml[:, :, r, :], in_=r_a[:]), reads=[b_r], writes=[b_hml])
        if r < 2:
            S.op("dve", lambda e, r=r: e.tensor_copy(out=r_b[:], in_=hml[:, :, r, :]), reads=[b_hml], writes=[b_r])
            S.op("dve", lambda e: e.tensor_tensor(out=r_a[:], in0=r_a[:], in1=r_b[:], op=ALU.subtract), reads=[b_r], writes=[b_r])
    S.dma("sp", lambda e: e.dma_start(out=aug_d.rearrange("h r (kb i) -> kb h r i", i=128), in_=hml[:]), reads=[b_hml], writes=[b_aug_d])
    qk = [(p.sb([67, SEQ], BF16, f"qaug{i}"), p.sb([67, SEQ], BF16, f"kaug{i}"), Buf(f"qk{i}")) for i in range(2)]
    for (qa, ka, b_qk) in qk:
        S.op("pool", lambda e, ka=ka: e.memset(ka[64:67, :], 1.0), writes=[b_qk])
    sc_banks = p.psum_banks[0:4]
    po_banks = p.psum_banks[4:6]
    pt_rot = Rot(p, 4, [128, 512], BF16, "pt")
    tmp_rot = Rot(p, 2, [128, 128], F32, "mtmp")
    rec_rot = Rot(p, 2, [64, 512], F32, "rec")
    o_rot = Rot(p, 2, [64, 512], BF16, "osb")
    b_out = Buf("out")
    steps = []
    for h in range(HPC):
        for qb in range(NQB):
            nkb = 4 * qb + 4
            for kb in range(nkb):
                steps.append((h, qb, kb, nkb))
    state = {"sc": 0, "po": 0}
    loaded = {}

    def ensure_head(h):
        if h in loaded:
            return loaded[h]
        qa, ka, b_qk = qk[h % 2]
        S.dma("sp", lambda e: e.dma_start(out=qa[0:64, :], in_=q4[h]), writes=[b_qk])
        S.dma("sp", lambda e: e.dma_start(out=ka[0:64, :], in_=k4[h]), writes=[b_qk])
        S.dma("sp", lambda e: e.dma_start(out=qa[64:67, :], in_=aug_d[h]), reads=[b_aug_d], writes=[b_qk])
        loaded[h] = (qa, ka, b_qk)
        return loaded[h]

    inflight = {}

    def stage1(i):
        h, qb, kb, nkb = steps[i]
        qa, ka, b_qk = ensure_head(h)
        j = kb - 4 * qb
        c0 = 128 * j if j > 0 else 0
        sc, b_sc = sc_banks[state["sc"] % 4]
        state["sc"] += 1
        S.op("pe", lambda e: e.matmul(sc[:, c0:512], lhsT=ka[0:67, kb * 128:(kb + 1) * 128], rhs=qa[0:67, qb * 512 + c0:(qb + 1) * 512], start=True, stop=True),
             reads=[b_qk], writes=[b_sc])
        pt, b_pt = pt_rot.next()
        bias = biascol[:, h, kb:kb + 1]
        if j < 0:
            S.op("act", lambda e: e.activation(out=pt[:, 0:512], in_=sc[:, 0:512], func=AF.Exp, bias=bias, scale=1.0), reads=[b_sc, b_bias], writes=[b_pt])
        else:
            tmp, b_tmp = tmp_rot.next()
            S.op("dve", lambda e: e.tensor_tensor(out=tmp[:], in0=sc[:, c0:c0 + 128], in1=maskneg[:], op=ALU.add), reads=[b_sc, b_c], writes=[b_tmp])
            S.op("act", lambda e: e.activation(out=pt[:, c0:c0 + 128], in_=tmp[:], func=AF.Exp, bias=bias, scale=1.0), reads=[b_tmp, b_bias], writes=[b_pt])
            if c0 + 128 < 512:
                S.op("act", lambda e: e.activation(out=pt[:, c0 + 128:512], in_=sc[:, c0 + 128:512], func=AF.Exp, bias=bias, scale=1.0), reads=[b_sc, b_bias], writes=[b_pt])
        inflight[i] = (pt, b_pt, c0)

    def stage2(i):
        h, qb, kb, nkb = steps[i]
        pt, b_pt, c0 = inflight.pop(i)
        if kb == 0:
            state["po"] += 1
        po, b_po = po_banks[state["po"] % 2]
        S.op("pe", lambda e: e.matmul(po[:, c0:512], lhsT=vaug[:, kb, h, :], rhs=pt[:, c0:512], start=(kb == 0), stop=(kb == nkb - 1)),
             reads=[b_v, b_pt], writes=[b_po], signal=(kb == nkb - 1))
        if kb == nkb - 1:
            rec, b_rec = rec_rot.next()
            S.op("dve", lambda e: e.reciprocal(out=rec[:], in_=po[64:128, :]), reads=[b_po], writes=[b_rec])
            osb, b_o = o_rot.next()
            S.op("dve", lambda e: e.tensor_tensor(out=osb[:], in0=po[0:64, :], in1=rec[:], op=ALU.mult), reads=[b_po, b_rec], writes=[b_o])
            S.dma("sp", lambda e: e.dma_start(out=oT[h * HD:(h + 1) * HD, qb * 512:(qb + 1) * 512], in_=osb[:]), reads=[b_o], writes=[b_out])

    LOOK = 2
    n = len(steps)
    for i in range(min(LOOK, n)):
        stage1(i)
    for i in range(n):
        if i + LOOK < n:
            stage1(i + LOOK)
        stage2(i)
    return p.finish([b_out])


SSD_G = 2
SSD_H = 8
SSD_P = 64
SSD_N = 128
SSD_L = 128
NCH = SEQ // SSD_L


def build_p3():
    p = Prog()
    S = p.S
    x_tok = p.dram_in("x_tok", [128, NCH, SSD_H * SSD_P], BF16)
    b_tok = p.dram_in("b_tok", [128, NCH, SSD_G * SSD_N], BF16)
    bT = p.dram_in("bT", [128, SSD_G, SEQ], BF16)
    cT = p.dram_in("cT", [128, SSD_G, SEQ], BF16)
    xsT = p.dram_in("xsT", [128, 4, SEQ], BF16)
    zT = p.dram_in("zT", [128, 4, SEQ], F32)
    dt_raw = p.dram_in("dt_raw", [128, NCH, SSD_H], F32)
    dtb = p.dram_in("dtb", [128, SSD_H], F32)
    alog = p.dram_in("alog", [128, SSD_H], F32)
    dcol = p.dram_in("dcol", [128, 4], F32)
    nwcol = p.dram_in("nwcol", [128, 4], F32)
    yT = p.dram_out("yT", [512, SEQ], BF16)
    p.init_common()
    b_c = Buf("c3")
    tri = p.sb([128, 128], F32, "tri")
    S.op("pool", lambda e: e.memset(tri[:], 1.0), writes=[b_c])
    S.op("pool", lambda e: e.affine_select(out=tri[:], in_=tri[:], pattern=[[1, 128]], compare_op=ALU.is_ge, fill=0.0, base=0, channel_multiplier=-1), reads=[b_c], writes=[b_c])
    onesf = p.sb([128, 128], F32, "onesf3")
    S.op("pool", lambda e: e.memset(onesf[:], 1.0), writes=[b_c])
    one_col = p.sb([128, 1], F32, "one_col3")
    S.op("pool", lambda e: e.memset(one_col[:], 1.0), writes=[b_c])
    dtb_sb, b1 = load_small(p, dtb, [128, SSD_H], "dtb_sb")
    alog_sb, b2 = load_small(p, alog, [128, SSD_H], "alog_sb")
    dcol_sb, b3 = load_small(p, dcol, [128, 4], "dcol_sb")
    nw_sb, b4 = load_small(p, nwcol, [128, 4], "nw_sb")
    dt = p.sb([128, NCH, SSD_H], F32, "dt"); b_dt = Buf("dt")
    adt = p.sb([128, NCH, SSD_H], F32, "adt"); b_adt = Buf("adt")
    S.dma("sp", lambda e: e.dma_start(out=dt[:], in_=dt_raw), writes=[b_dt])
    for c in range(NCH):
        S.op("dve", lambda e, c=c: e.tensor_tensor(out=dt[:, c, :], in0=dt[:, c, :], in1=dtb_sb[:], op=ALU.add), reads=[b_dt, b1], writes=[b_dt])
    dtf = dt[:].rearrange("p c h -> p (c h)")
    S.op("act", lambda e: e.activation(out=dtf, in_=dtf, func=AF.Exp), reads=[b_dt], writes=[b_dt])
    S.op("act", lambda e: e.activation(out=dtf, in_=dtf, func=AF.Ln, bias=one_col[:], scale=1.0), reads=[b_dt, b_c], writes=[b_dt])
    nega = p.sb([128, SSD_H], F32, "nega"); b_na = Buf("nega")
    S.op("act", lambda e: e.activation(out=nega[:], in_=alog_sb[:], func=AF.Exp), reads=[b2], writes=[b_na])
    S.op("dve", lambda e: e.tensor_scalar(out=nega[:], in0=nega[:], scalar1=-1.0, scalar2=0.0, op0=ALU.mult, op1=ALU.add), reads=[b_na], writes=[b_na])
    for c in range(NCH):
        S.op("dve", lambda e, c=c: e.tensor_tensor(out=adt[:, c, :], in0=dt[:, c, :], in1=nega[:], op=ALU.mult), reads=[b_dt, b_na], writes=[b_adt])
    ST = p.sb([128, SSD_G, 256], F32, "ST"); b_ST = Buf("ST")
    STb = p.sb([128, SSD_G, 256], BF16, "STb"); b_STb = Buf("STb")
    S.op("pool", lambda e: e.memset(ST[:], 0.0), writes=[b_ST])
    S.op("pool", lambda e: e.memset(STb[:], 0.0), writes=[b_STb])
    xt_rot = Rot(p, 2, [128, SSD_H * SSD_P], BF16, "xt")
    bt_rot = Rot(p, 2, [128, SSD_G * SSD_N], BF16, "btk")
    bT_rot = Rot(p, 2, [128, SSD_G, 128], BF16, "bTc")
    cT_rot = Rot(p, 2, [128, SSD_G, 128], BF16, "cTc")
    xs_rot = Rot(p, 2, [128, 4, 128], BF16, "xsc")
    z_rot = Rot(p, 2, [128, 4, 128], F32, "zc")
    sm_rot = Rot(p, 2, [128, 4, SSD_H], F32, "sm")
    trih_rot = Rot(p, 2, [128, 128], F32, "trih")
    t1_rot = Rot(p, 2, [128, 128], F32, "t1")
    er_rot = Rot(p, 2, [128, 128], F32, "er")
    wt_rot = Rot(p, 2, [128, 128], BF16, "wt")
    cs_rot = Rot(p, 2, [128, 128], BF16, "cs3")
    cbm_rot = Rot(p, 2, [128, SSD_G, 128], F32, "cbm")
    xw_rot = Rot(p, 2, [128, SSD_H * SSD_P], BF16, "xw")
    y_rot = Rot(p, 2, [128, 4, 128], F32, "ych")
    sq_rot = Rot(p, 2, [128, 4, 128], BF16, "ysq")
    rs_rot = Rot(p, 2, [128, SSD_G, 128], F32, "yrs")
    yo_rot = Rot(p, 2, [128, 4, 128], BF16, "yo")
    b_out = Buf("yout")
    for c in range(NCH):
        cols = slice(c * 128, (c + 1) * 128)
        xt, b_xt = xt_rot.next(); btk, b_btk = bt_rot.next(); bTc, b_bT = bT_rot.next(); cTc, b_cT = cT_rot.next()
        xsc, b_xs = xs_rot.next(); zc, b_z = z_rot.next()
        S.dma("sp", lambda e, xt=xt, c=c: e.dma_start(out=xt[:], in_=x_tok[:, c, :]), writes=[b_xt])
        S.dma("sp", lambda e, btk=btk, c=c: e.dma_start(out=btk[:], in_=b_tok[:, c, :]), writes=[b_btk])
        S.dma("sp", lambda e, bTc=bTc, cols=cols: e.dma_start(out=bTc[:], in_=bT[:, :, cols]), writes=[b_bT])
        S.dma("sp", lambda e, cTc=cTc, cols=cols: e.dma_start(out=cTc[:], in_=cT[:, :, cols]), writes=[b_cT])
        S.dma("sp", lambda e, xsc=xsc, cols=cols: e.dma_start(out=xsc[:], in_=xsT[:, :, cols]), writes=[b_xs])
        S.dma("sp", lambda e, zc=zc, cols=cols: e.dma_start(out=zc[:], in_=zT[:, :, cols]), writes=[b_z])
        sm, b_sm = sm_rot.next()
        bk, b_bk = p.bank()
        S.op("pe", lambda e, bk=bk, c=c: e.matmul(bk[:, 0:SSD_H], lhsT=tri[:], rhs=adt[:, c, :], start=True, stop=True), reads=[b_c, b_adt], writes=[b_bk])
        S.op("pe", lambda e, bk=bk, c=c: e.matmul(bk[:, 8:8 + SSD_H], lhsT=onesf[:], rhs=adt[:, c, :], start=True, stop=True), reads=[b_c, b_adt], writes=[b_bk])
        S.op("dve", lambda e, sm=sm, bk=bk: e.tensor_copy(out=sm[:, 0:2, :], in_=bk[:, 0:16].rearrange("p (a h) -> p a h", a=2)), reads=[b_bk], writes=[b_sm])
        acum = sm[:, 0, :]; tot = sm[:, 1, :]; wcol = sm[:, 2, :]; dec = sm[:, 3, :]
        S.op("dve", lambda e, wcol=wcol, tot=tot, acum=acum: e.tensor_tensor(out=wcol, in0=tot, in1=acum, op=ALU.subtract), reads=[b_sm], writes=[b_sm])
        S.op("act", lambda e, wcol=wcol: e.activation(out=wcol, in_=wcol, func=AF.Exp), reads=[b_sm], writes=[b_sm])
        S.op("dve", lambda e, wcol=wcol, c=c: e.tensor_tensor(out=wcol, in0=wcol, in1=dt[:, c, :], op=ALU.mult), reads=[b_sm, b_dt], writes=[b_sm])
        S.op("act", lambda e, dec=dec, tot=tot: e.activation(out=dec, in_=tot, func=AF.Exp), reads=[b_sm], writes=[b_sm])
        cbm, b_cbm = cbm_rot.next()
        for g in range(SSD_G):
            bk2, b_bk2 = p.bank()
            S.op("pe", lambda e, bk2=bk2, g=g, bTc=bTc, cTc=cTc: e.matmul(bk2[:, 0:128], lhsT=bTc[:, g, :], rhs=cTc[:, g, :], start=True, stop=True), reads=[b_bT, b_cT], writes=[b_bk2])
            S.op("dve", lambda e, bk2=bk2, g=g, cbm=cbm: e.tensor_tensor(out=cbm[:, g, :], in0=bk2[:, 0:128], in1=tri[:], op=ALU.mult), reads=[b_bk2, b_c], writes=[b_cbm])
        ych, b_y = y_rot.next()
        for h in range(SSD_H):
            g = h // 4
            trih, b_trih = trih_rot.next()
            S.op("dve", lambda e, trih=trih, c=c, h=h: e.tensor_scalar(out=trih[:], in0=tri[:], scalar1=adt[:, c, h:h + 1], scalar2=0.0, op0=ALU.mult, op1=ALU.add), reads=[b_c, b_adt], writes=[b_trih])
            bk3, b_bk3 = p.bank()
            S.op("pe", lambda e, bk3=bk3, trih=trih: e.matmul(bk3[:, 0:128], lhsT=onesf[:], rhs=trih[:], start=True, stop=True), reads=[b_c, b_trih], writes=[b_bk3])
            t1, b_t1 = t1_rot.next()
            S.op("dve", lambda e, t1=t1, bk3=bk3, acum=acum, h=h: e.tensor_scalar(out=t1[:], in0=bk3[:, 0:128], scalar1=acum[:, h:h + 1], scalar2=0.0, op0=ALU.subtract, op1=ALU.min), reads=[b_bk3, b_sm], writes=[b_t1])
            S.op("act", lambda e, t1=t1: e.activation(out=t1[:], in_=t1[:], func=AF.Exp), reads=[b_t1], writes=[b_t1])
            wt, b_wt = wt_rot.next()
            S.op("dve", lambda e, wt=wt, t1=t1, c=c, h=h, cbm=cbm, g=g: e.scalar_tensor_tensor(out=wt[:], in0=t1[:], scalar=dt[:, c, h:h + 1], in1=cbm[:, g, :], op0=ALU.mult, op1=ALU.mult),
                 reads=[b_t1, b_dt, b_cbm], writes=[b_wt])
            er, b_er = er_rot.next()
            S.op("act", lambda e, er=er, bk3=bk3: e.activation(out=er[:], in_=bk3[:, 0:128], func=AF.Exp), reads=[b_bk3], writes=[b_er])
            cs, b_cs = cs_rot.next()
            S.op("dve", lambda e, cs=cs, er=er, cTc=cTc, g=g: e.tensor_tensor(out=cs[:], in0=er[:], in1=cTc[:, g, :], op=ALU.mult), reads=[b_er, b_cT], writes=[b_cs])
            bk4, b_bk4 = p.bank()
            S.op("pe", lambda e, bk4=bk4, xt=xt, wt=wt, h=h: e.matmul(bk4[0:64, 0:128], lhsT=xt[:, h * 64:(h + 1) * 64], rhs=wt[:], start=True, stop=False), reads=[b_xt, b_wt], writes=[b_bk4], signal=False)
            S.op("pe", lambda e, bk4=bk4, cs=cs, h=h, g=g: e.matmul(bk4[0:64, 0:128], lhsT=STb[:, g, (h % 4) * 64:(h % 4 + 1) * 64], rhs=cs[:], start=False, stop=True), reads=[b_STb, b_cs], writes=[b_bk4])
            po = (h % 2) * 64
            S.op("act", lambda e, ych=ych, bk4=bk4, h=h, po=po: e.activation(out=ych[po:po + 64, h // 2, :], in_=bk4[0:64, 0:128], func=AF.Copy), reads=[b_bk4], writes=[b_y])
        xw, b_xw = xw_rot.next()
        for h in range(SSD_H):
            S.op("dve", lambda e, xw=xw, xt=xt, wcol=wcol, h=h: e.tensor_scalar(out=xw[:, h * 64:(h + 1) * 64], in0=xt[:, h * 64:(h + 1) * 64], scalar1=wcol[:, h:h + 1], scalar2=0.0, op0=ALU.mult, op1=ALU.add),
                 reads=[b_xt, b_sm], writes=[b_xw])
        for g in range(SSD_G):
            bk5, b_bk5 = p.bank()
            S.op("pe", lambda e, bk5=bk5, btk=btk, xw=xw, g=g: e.matmul(bk5[:, 0:256], lhsT=btk[:, g * 128:(g + 1) * 128], rhs=xw[:, g * 256:(g + 1) * 256], start=True, stop=True), reads=[b_btk, b_xw], writes=[b_bk5])
            for k in range(4):
                h = g * 4 + k
                S.op("dve", lambda e, bk5=bk5, g=g, k=k, h=h, dec=dec: e.scalar_tensor_tensor(out=ST[:, g, k * 64:(k + 1) * 64], in0=ST[:, g, k * 64:(k + 1) * 64], scalar=dec[:, h:h + 1], in1=bk5[:, k * 64:(k + 1) * 64], op0=ALU.mult, op1=ALU.add),
                     reads=[b_bk5, b_sm, b_ST, b_STb], writes=[b_ST])
        S.op("act", lambda e: e.activation(out=STb[:], in_=ST[:], func=AF.Copy), reads=[b_ST], writes=[b_STb])
        for j in range(4):
            S.op("dve", lambda e, ych=ych, xsc=xsc, j=j: e.scalar_tensor_tensor(out=ych[:, j, :], in0=xsc[:, j, :], scalar=dcol_sb[:, j:j + 1], in1=ych[:, j, :], op0=ALU.mult, op1=ALU.add),
                 reads=[b_xs, b3, b_y], writes=[b_y])
        S.op("act", lambda e, zc=zc: e.activation(out=zc[:], in_=zc[:], func=AF.Silu), reads=[b_z], writes=[b_z])
        S.op("dve", lambda e, ych=ych, zc=zc: e.tensor_tensor(out=ych[:], in0=ych[:], in1=zc[:], op=ALU.mult), reads=[b_y, b_z], writes=[b_y])
        sq, b_sq = sq_rot.next()
        S.op("act", lambda e, sq=sq, ych=ych: e.activation(out=sq[:], in_=ych[:], func=AF.Square), reads=[b_y], writes=[b_sq])
        rs, b_rs = rs_rot.next()
        for g in range(SSD_G):
            bk6, b_bk6 = p.bank()
            for jj in range(2):
                S.op("pe", lambda e, bk6=bk6, sq=sq, g=g, jj=jj: e.matmul(bk6[:, 0:128], lhsT=p.ones[:], rhs=sq[:, 2 * g + jj, :], start=(jj == 0), stop=(jj == 1)), reads=[p.b_const, b_sq], writes=[b_bk6], signal=(jj == 1))
            S.op("act", lambda e, rs=rs, bk6=bk6, g=g: e.activation(out=rs[:, g, :], in_=bk6[:, 0:128], func=AF.Sqrt, bias=p.eps_t[:], scale=1.0 / 256), reads=[b_bk6, p.b_const], writes=[b_rs])
        S.op("dve", lambda e, rs=rs: e.reciprocal(out=rs[:], in_=rs[:]), reads=[b_rs], writes=[b_rs])
        yo, b_yo = yo_rot.next()
        for j in range(4):
            S.op("dve", lambda e, yo=yo, ych=ych, rs=rs, j=j: e.scalar_tensor_tensor(out=yo[:, j, :], in0=ych[:, j, :], scalar=nw_sb[:, j:j + 1], in1=rs[:, j // 2, :], op0=ALU.mult, op1=ALU.mult),
                 reads=[b_y, b4, b_rs], writes=[b_yo])
        S.dma("sp", lambda e, yo=yo, cols=cols: e.dma_start(out=yT[:, cols].rearrange("(j q) t -> q j t", q=128), in_=yo[:]), reads=[b_yo], writes=[b_out])
    return p.finish([b_out])


SSD_IN = 6176
SSD_DI = 2048
SSD_CD = 4096


def build_p2():
    p = Prog()
    S = p.S
    HAL = 3
    x1T = p.dram_in("x1T", [D, T_CORE], F32)
    x1h = p.dram_in("x1h", [D, HAL], F32)
    aT = p.dram_in("aT", [D, T_CORE], BF16)
    ah = p.dram_in("ah", [D, HAL], BF16)
    gcols = p.dram_in("gcols", [128, 2, NKC], F32)
    w_o = p.dram_in("w_o", [D, D], F32)
    w_gu = p.dram_in("w_gu", [D, 2 * DFF], F32)
    w_down = p.dram_in("w_down", [DFF, D], F32)
    w_in = p.dram_in("w_in", [D, SSD_IN], F32)
    cw = p.dram_in("cw", [128, 32, 4], F32)
    cb = p.dram_in("cb", [128, 32], F32)
    x2T = p.dram_out("x2T", [D, T_CORE], F32)
    zT = p.dram_out("zT", [SSD_DI, T_CORE], F32)
    xbcT = p.dram_out("xbcT", [SSD_CD, T_CORE], BF16)
    dtT = p.dram_out("dtT", [32, T_CORE], F32)
    p.init_common()
    ts = TokState(p)
    g_sb, b_g = load_small(p, gcols, [128, 2, NKC], "g_sb")
    cw_sb, b_cw = load_small(p, cw, [128, 32, 4], "cw_sb")
    cb_sb, b_cb = load_small(p, cb, [128, 32], "cb_sb")
    carry = p.sb([128, 32, HAL], F32, "carry"); b_carry = Buf("carry")
    zo_rot = Rot(p, 2, [128, 512], F32, "zo")
    xo_rot = Rot(p, 2, [128, TB], BF16, "xo")
    b_out = Buf("out2")
    for ps_ in range(T_CORE // TB):
        hal = HAL if ps_ == 0 else 0
        ntok = TB + hal
        tok0 = ps_ * TB
        tiles = [(t0, min(512, ntok - t0)) for t0 in range(0, ntok, 512)]
        if hal:
            S.dma("sp", lambda e: e.dma_start(out=ts.x[:, :, 0:HAL], in_=x1h.rearrange("(c q) t -> q c t", q=128)), writes=[ts.b_x])
            S.dma("sp", lambda e: e.dma_start(out=ts.zh[:, :, 0:HAL], in_=ah.rearrange("(c q) t -> q c t", q=128)), writes=[ts.b_zh])
        S.dma("sp", lambda e, hal=hal, tok0=tok0: e.dma_start(out=ts.x[:, :, hal:hal + TB], in_=x1T[:, tok0:tok0 + TB].rearrange("(c q) t -> q c t", q=128)), writes=[ts.b_x])
        S.dma("sp", lambda e, hal=hal, tok0=tok0: e.dma_start(out=ts.zh[:, :, hal:hal + TB], in_=aT[:, tok0:tok0 + TB].rearrange("(c q) t -> q c t", q=128)), writes=[ts.b_zh])
        outproj_pass(p, ts.wlr, ts.x, ts.b_x, ts.zh, ts.b_zh, w_o, NKC, ntok)
        rmsnorm(p, ts.x, ts.b_x, g_sb[:, 0, :], b_g, ts.h, ts.b_h, ntok, ts.sq_rot, ts.rstd_rot)
        ffn_pass(p, ts.wl, ts.wlr, ts.x, ts.b_x, ts.h, ts.b_h, w_gu, w_down, ntok, ts.hid, ts.b_zh, ts.silu_rot)
        S.dma("sp", lambda e, hal=hal, tok0=tok0: e.dma_start(out=x2T[:, tok0:tok0 + TB].rearrange("(c q) t -> q c t", q=128), in_=ts.x[:, :, hal:hal + TB]), reads=[ts.b_x], writes=[b_out])
        rmsnorm(p, ts.x, ts.b_x, g_sb[:, 1, :], b_g, ts.h, ts.b_h, ntok, ts.sq_rot, ts.rstd_rot)
        for m in range(48):
            wbf, b_w = ts.wl.load([w_in[:, m * 128:(m + 1) * 128].rearrange("(k q) n -> q k n", q=128)])
            if m >= 16:
                pre, b_pre = ts.cv_rot.next()
                mc = m - 16
                if not hal:
                    S.op("pool", lambda e, pre=pre, mc=mc: e.tensor_copy(out=pre[:, 0:HAL], in_=carry[:, mc, :]), reads=[b_carry], writes=[b_pre])
            for (t0, n) in tiles:
                pq, b_pq = gemm_acc(p, wbf, b_w, lambda k: wbf[:, k * 128:(k + 1) * 128], lambda k: ts.h[:, k, t0:t0 + n], NKC, n, [ts.b_h])
                if m < 16:
                    a0 = max(t0, hal)
                    if a0 >= t0 + n:
                        continue
                    zo, b_zo = zo_rot.next()
                    S.op("act", lambda e, zo=zo, pq=pq, n=n: e.activation(out=zo[:, 0:n], in_=pq[:, 0:n], func=AF.Copy), reads=[b_pq], writes=[b_zo])
                    S.dma("sp", lambda e, zo=zo, m=m, a0=a0, t0=t0, n=n, hal=hal, tok0=tok0: e.dma_start(
                        out=zT[m * 128:(m + 1) * 128, tok0 + a0 - hal:tok0 + t0 + n - hal], in_=zo[:, a0 - t0:n]), reads=[b_zo], writes=[b_out])
                else:
                    S.op("act", lambda e, pre=pre, pq=pq, t0=t0, n=n, hal=hal: e.activation(out=pre[:, HAL - hal + t0:HAL - hal + t0 + n], in_=pq[:, 0:n], func=AF.Copy),
                         reads=[b_pq], writes=[b_pre])
            if m >= 16:
                u, b_u = ts.u_rot.next()
                S.op("pool", lambda e, u=u, pre=pre, mc=mc: e.tensor_scalar(out=u[:, 0:TB], in0=pre[:, 0:TB], scalar1=cw_sb[:, mc, 0:1], scalar2=None, op0=ALU.mult), reads=[b_pre, b_cw], writes=[b_u])
                for tap in (1, 2, 3):
                    S.op("dve", lambda e, u=u, pre=pre, mc=mc, tap=tap: e.scalar_tensor_tensor(out=u[:, 0:TB], in0=pre[:, tap:tap + TB], scalar=cw_sb[:, mc, tap:tap + 1], in1=u[:, 0:TB], op0=ALU.mult, op1=ALU.add),
                         reads=[b_pre, b_cw, b_u], writes=[b_u])
                S.op("pool", lambda e, pre=pre, mc=mc: e.tensor_copy(out=carry[:, mc, :], in_=pre[:, TB:TB + HAL]), reads=[b_pre], writes=[b_carry])
                xo, b_xo = xo_rot.next()
                S.op("act", lambda e, xo=xo, u=u, mc=mc: e.activation(out=xo[:, 0:TB], in_=u[:, 0:TB], func=AF.Silu, bias=cb_sb[:, mc:mc + 1], scale=1.0), reads=[b_u, b_cb], writes=[b_xo])
                S.dma("sp", lambda e, xo=xo, mc=mc, tok0=tok0: e.dma_start(out=xbcT[mc * 128:(mc + 1) * 128, tok0:tok0 + TB], in_=xo[:, 0:TB]), reads=[b_xo], writes=[b_out])
        wbf, b_w = ts.wl.load([w_in[:, 6144:6176].rearrange("(k q) n -> q k n", q=128)])
        for (t0, n) in tiles:
            a0 = max(t0, hal)
            if a0 >= t0 + n:
                continue
            pq, b_pq = gemm_acc(p, wbf, b_w, lambda k: wbf[:, k * 32:(k + 1) * 32], lambda k: ts.h[:, k, t0:t0 + n], NKC, n, [ts.b_h])
            zo, b_zo = zo_rot.next()
            S.op("act", lambda e, zo=zo, pq=pq, n=n: e.activation(out=zo[0:32, 0:n], in_=pq[0:32, 0:n], func=AF.Copy), reads=[b_pq], writes=[b_zo])
            S.dma("sp", lambda e, zo=zo, a0=a0, t0=t0, n=n, hal=hal, tok0=tok0: e.dma_start(
                out=dtT[:, tok0 + a0 - hal:tok0 + t0 + n - hal], in_=zo[0:32, a0 - t0:n]), reads=[b_zo], writes=[b_out])
    return p.finish([b_out])


def build_p4():
    p = Prog()
    S = p.S
    HAL = 2
    x2T = p.dram_in("x2T", [D, T_CORE], F32)
    x2h = p.dram_in("x2h", [D, HAL], F32)
    yT = p.dram_in("yT", [SSD_DI, T_CORE], BF16)
    yh = p.dram_in("yh", [SSD_DI, HAL], BF16)
    gcols = p.dram_in("gcols", [128, 3, NKC], F32)
    dwc = p.dram_in("dwc", [128, 3, NKC], F32)
    w_so = p.dram_in("w_so", [SSD_DI, D], F32)
    w_gu2 = p.dram_in("w_gu2", [D, 2 * DFF], F32)
    w_down2 = p.dram_in("w_down2", [DFF, D], F32)
    w_in = p.dram_in("w_in", [D, 3 * D], F32)
    w_out = p.dram_in("w_out", [D, D], F32)
    w_gu3 = p.dram_in("w_gu3", [D, 2 * DFF], F32)
    w_down3 = p.dram_in("w_down3", [DFF, D], F32)
    outT = p.dram_out("outT", [D, T_CORE], F32)
    p.init_common()
    ts = TokState(p)
    g_sb, b_g = load_small(p, gcols, [128, 3, NKC], "g_sb")
    dw_sb, b_dw = load_small(p, dwc, [128, 3, NKC], "dw_sb")
    cvh = [(p.sb([128, NKC, 2], F32, f"cvh{i}"), Buf()) for i in range(2)]
    S.op("pool", lambda e: e.memset(cvh[0][0][:], 0.0), writes=[cvh[0][1]])
    b_out = Buf("out4")
    for ps_ in range(T_CORE // TB):
        hal = HAL if ps_ == 0 else 0
        ntok = TB + hal
        tok0 = ps_ * TB
        if hal:
            S.dma("sp", lambda e: e.dma_start(out=ts.x[:, :, 0:HAL], in_=x2h.rearrange("(c q) t -> q c t", q=128)), writes=[ts.b_x])
        S.dma("sp", lambda e, hal=hal, tok0=tok0: e.dma_start(out=ts.x[:, :, hal:hal + TB], in_=x2T[:, tok0:tok0 + TB].rearrange("(c q) t -> q c t", q=128)), writes=[ts.b_x])
        for half in range(2):
            if hal:
                S.dma("sp", lambda e, half=half: e.dma_start(out=ts.zh[:, :, 0:HAL], in_=yh[half * D:(half + 1) * D, :].rearrange("(c q) t -> q c t", q=128)), writes=[ts.b_zh])
            S.dma("sp", lambda e, half=half, hal=hal, tok0=tok0: e.dma_start(out=ts.zh[:, :, hal:hal + TB], in_=yT[half * D:(half + 1) * D, tok0:tok0 + TB].rearrange("(c q) t -> q c t", q=128)), writes=[ts.b_zh])
            outproj_pass(p, ts.wlr, ts.x, ts.b_x, ts.zh, ts.b_zh, w_so[half * D:(half + 1) * D, :], NKC, ntok)
        rmsnorm(p, ts.x, ts.b_x, g_sb[:, 0, :], b_g, ts.h, ts.b_h, ntok, ts.sq_rot, ts.rstd_rot)
        ffn_pass(p, ts.wl, ts.wlr, ts.x, ts.b_x, ts.h, ts.b_h, w_gu2, w_down2, ntok, ts.hid, ts.b_zh, ts.silu_rot)
        rmsnorm(p, ts.x, ts.b_x, g_sb[:, 1, :], b_g, ts.h, ts.b_h, ntok, ts.sq_rot, ts.rstd_rot)
        ci, co = cvh[ps_ % 2], cvh[(ps_ + 1) % 2]
        conv_mixer_pass(p, ts.wl, ts.h, ts.b_h, w_in, dw_sb, b_dw, ntok, ci[0], ci[1], co[0], co[1], ts.zh, ts.b_zh, ts.cv_rot, ts.u_rot, ts.c_rot)
        outproj_pass(p, ts.wlr, ts.x, ts.b_x, ts.zh, ts.b_zh, w_out, NKC, ntok)
        rmsnorm(p, ts.x, ts.b_x, g_sb[:, 2, :], b_g, ts.h, ts.b_h, ntok, ts.sq_rot, ts.rstd_rot)
        ffn_pass(p, ts.wl, ts.wlr, ts.x, ts.b_x, ts.h, ts.b_h, w_gu3, w_down3, ntok, ts.hid, ts.b_zh, ts.silu_rot)
        S.dma("sp", lambda e, hal=hal, tok0=tok0: e.dma_start(out=outT[:, tok0:tok0 + TB].rearrange("(c q) t -> q c t", q=128), in_=ts.x[:, :, hal:hal + TB]), reads=[ts.b_x], writes=[b_out])
    return p.finish([b_out])


BF = ml_dtypes.bfloat16
NCORE = 8
_PROGS = {}


def _prog(name, builder):
    if name not in _PROGS:
        _PROGS[name] = builder()
    return _PROGS[name]


def _run(name, builder, maps):
    nc = _prog(name, builder)
    res = run_bass_kernel_spmd(nc, maps, core_ids=list(range(NCORE)))
    return res.results


def _c(a):
    return np.ascontiguousarray(a)


def kernel(x, mix_norm, ffn_norm, ffn_w_gu, ffn_w_down, conv_w_in, conv_w_dw, conv_w_out,
           fox_w_in, fox_b_f, fox_q_gain, fox_k_gain, fox_w_out,
           ssd_w_in, ssd_conv_w, ssd_conv_b, ssd_dt_bias, ssd_a_log, ssd_d, ssd_norm_w, ssd_w_out):
    f32 = np.float32
    x = np.asarray(x, f32)
    T = T_CORE
    QPB = SEQ // T

    def dwcols(w):
        return _c(np.asarray(w, f32).reshape(3, NKC, 128).transpose(2, 0, 1))

    maps = []
    for c in range(NCORE):
        b, q = divmod(c, QPB)
        xs = x[b, q * T:(q + 1) * T]
        xh = x[b, q * T - 2:q * T] if q > 0 else np.zeros((2, D), f32)
        maps.append({
            "xT": _c(xs.T), "xh": _c(xh.T),
            "gcols": _c(np.stack([gcol(mix_norm[0]), gcol(ffn_norm[0]), gcol(mix_norm[1])], axis=1)),
            "dwc": dwcols(conv_w_dw[0]),
            "w_in": _c(conv_w_in[0]), "w_out": _c(conv_w_out[0]), "w_gu": _c(ffn_w_gu[0]), "w_down": _c(ffn_w_down[0]),
            "fox_w_in": _c(fox_w_in[0]),
            "gq": _c(np.tile(np.asarray(fox_q_gain[0], f32), 2)[:, None]),
            "gk": _c(np.tile(np.asarray(fox_k_gain[0], f32), 2)[:, None]),
            "bf": _c(np.asarray(fox_b_f[0], f32)[:, None]),
        })
    r0 = _run("p0", build_p0, maps)
    x1T = [r0[c]["x1T"] for c in range(NCORE)]

    maps = []
    for c in range(NCORE):
        b, hg = divmod(c, 4)
        cs = [b * QPB + i for i in range(QPB)]
        Q = np.concatenate([r0[i]["qT"][hg * 256:(hg + 1) * 256] for i in cs], axis=1)
        Kt = np.concatenate([r0[i]["kT"][hg * 256:(hg + 1) * 256] for i in cs], axis=1)
        V = np.concatenate([r0[i]["vT"][hg * 256:(hg + 1) * 256] for i in cs], axis=1)
        NL = np.concatenate([r0[i]["nlf"][hg * 4:(hg + 1) * 4] for i in cs], axis=1)
        maps.append({
            "q4": _c(Q.reshape(HPC, HD, SEQ)), "k4": _c(Kt.reshape(HPC, HD, SEQ)),
            "v4": _c(V.T.reshape(NKB, 128, HPC, HD).transpose(1, 0, 2, 3)),
            "nlf_blk": _c(NL.reshape(HPC, NKB, 128).transpose(1, 0, 2)),
        })
    r1 = _run("p1b", build_p1b, maps)

    cwc = _c(np.asarray(ssd_conv_w[0], f32).reshape(4, 32, 128).transpose(2, 1, 0))
    cbc = _c(np.asarray(ssd_conv_b[0], f32).reshape(32, 128).T)
    maps = []
    for c in range(NCORE):
        b, q = divmod(c, QPB)
        A = np.concatenate([r1[b * 4 + hg]["oT"] for hg in range(4)], axis=0)
        aT = A[:, q * T:(q + 1) * T]
        if q > 0:
            ah = A[:, q * T - 3:q * T]
            x1h = x1T[c - 1][:, T - 3:T]
        else:
            ah = np.zeros((D, 3), BF)
            x1h = np.zeros((D, 3), f32)
        maps.append({
            "x1T": x1T[c], "x1h": _c(x1h), "aT": _c(aT), "ah": _c(ah),
            "gcols": _c(np.stack([gcol(ffn_norm[1]), gcol(mix_norm[2])], axis=1)),
            "w_o": _c(fox_w_out[0]), "w_gu": _c(ffn_w_gu[1]), "w_down": _c(ffn_w_down[1]),
            "w_in": _c(ssd_w_in[0]), "cw": cwc, "cb": cbc,
        })
    r2 = _run("p2", build_p2, maps)
    x2T = [r2[c]["x2T"] for c in range(NCORE)]

    maps = []
    for c in range(NCORE):
        b, gp = divmod(c, 4)
        cs = [b * QPB + i for i in range(QPB)]
        XBC = np.concatenate([r2[i]["xbcT"] for i in cs], axis=1)
        Z = np.concatenate([r2[i]["zT"][gp * 512:(gp + 1) * 512] for i in cs], axis=1)
        DT = np.concatenate([r2[i]["dtT"][gp * 8:(gp + 1) * 8] for i in cs], axis=1)
        XS = XBC[gp * 512:(gp + 1) * 512]
        Bt = XBC[2048 + gp * 256:2048 + (gp + 1) * 256]
        Ct = XBC[3072 + gp * 256:3072 + (gp + 1) * 256]
        maps.append({
            "x_tok": _c(XS.T.reshape(NCH, 128, 512).transpose(1, 0, 2)),
            "b_tok": _c(Bt.T.reshape(NCH, 128, 256).transpose(1, 0, 2)),
            "bT": _c(Bt.reshape(2, 128, SEQ).transpose(1, 0, 2)),
            "cT": _c(Ct.reshape(2, 128, SEQ).transpose(1, 0, 2)),
            "xsT": _c(XS.reshape(4, 128, SEQ).transpose(1, 0, 2)),
            "zT": _c(Z.reshape(4, 128, SEQ).transpose(1, 0, 2)),
            "dt_raw": _c(DT.T.reshape(NCH, 128, 8).transpose(1, 0, 2)),
            "dtb": _c(np.tile(np.asarray(ssd_dt_bias[0], f32)[gp * 8:(gp + 1) * 8][None], (128, 1))),
            "alog": _c(np.tile(np.asarray(ssd_a_log[0], f32)[gp * 8:(gp + 1) * 8][None], (128, 1))),
            "dcol": _c(np.repeat(np.asarray(ssd_d[0], f32)[gp * 8:(gp + 1) * 8], 64).reshape(4, 128).T),
            "nwcol": _c(np.asarray(ssd_norm_w[0], f32)[gp * 512:(gp + 1) * 512].reshape(4, 128).T),
        })
    r3 = _run("p3", build_p3, maps)

    maps = []
    for c in range(NCORE):
        b, q = divmod(c, QPB)
        Y = np.concatenate([r3[b * 4 + gp]["yT"] for gp in range(4)], axis=0)
        if q > 0:
            yh = Y[:, q * T - 2:q * T]
            x2h = x2T[c - 1][:, T - 2:T]
        else:
            yh = np.zeros((SSD_DI, 2), BF)
            x2h = np.zeros((D, 2), f32)
        maps.append({
            "x2T": x2T[c], "x2h": _c(x2h), "yT": _c(Y[:, q * T:(q + 1) * T]), "yh": _c(yh),
            "gcols": _c(np.stack([gcol(ffn_norm[2]), gcol(mix_norm[3]), gcol(ffn_norm[3])], axis=1)),
            "dwc": dwcols(conv_w_dw[1]),
            "w_so": _c(ssd_w_out[0]), "w_gu2": _c(ffn_w_gu[2]), "w_down2": _c(ffn_w_down[2]),
            "w_in": _c(conv_w_in[1]), "w_out": _c(conv_w_out[1]), "w_gu3": _c(ffn_w_gu[3]), "w_down3": _c(ffn_w_down[3]),
        })
    r4 = _run("p4", build_p4, maps)
    out = np.empty((2, SEQ, D), f32)
    for c in range(NCORE):
        b, q = divmod(c, QPB)
        out[b, q * T:(q + 1) * T] = r4[c]["outT"].T
    return out
```
